# Optimizing a Trainium2 kernel written in Bass

```python
import jax, jax.numpy as jnp
from jax import lax
import numpy as np


D_MODEL = 1024
BATCH = 2
SEQ = 8192
DEPTH = 1

N_META = 16
CHUNK = 128
FRONT = CHUNK - N_META
EPS = 1e-6

SSD_EXPAND = 2
SSD_D_INNER = SSD_EXPAND * D_MODEL
SSD_HEAD_DIM = 64
SSD_HEADS = SSD_D_INNER // SSD_HEAD_DIM
SSD_GROUPS = 4
SSD_STATE = 128
SSD_CONV = 4
SSD_CONV_DIM = SSD_D_INNER + 2 * SSD_GROUPS * SSD_STATE

RET_HEADS = 4
RET_QK_DIM = D_MODEL
RET_V_DIM = 2 * D_MODEL
RET_HEAD_QK = RET_QK_DIM // RET_HEADS
RET_HEAD_V = RET_V_DIM // RET_HEADS
ROPE_BASE = 10000.0

D_FF = ((8 * D_MODEL // 3 + 127) // 128) * 128
FFN_CONV = 3

IN_PROJ_DIM = SSD_D_INNER + SSD_CONV_DIM + SSD_HEADS + 2 * RET_QK_DIM + 2 * RET_V_DIM + 2 * D_MODEL

kernel_name = 'hybrid_ssd_retention_meta_block'


def _split_points():
    widths = [SSD_D_INNER, SSD_CONV_DIM, SSD_HEADS, RET_QK_DIM, RET_QK_DIM,
              RET_V_DIM, RET_V_DIM, D_MODEL, D_MODEL]
    pts = []
    acc = 0
    for w in widths[:-1]:
        acc += w
        pts.append(acc)
    return pts


def rmsnorm(x, w):
    xf = x.astype(jnp.float32)
    y = xf * lax.rsqrt(jnp.mean(xf * xf, axis=-1, keepdims=True) + EPS)
    return (y * w.astype(jnp.float32)).astype(x.dtype)


def causal_dwconv(x, w, b):
    k_width = w.shape[0]
    seq_len = x.shape[1]
    xp = jnp.pad(x, ((0, 0), (k_width - 1, 0), (0, 0)))
    y = b + xp[:, 0:seq_len] * w[0]
    for k in range(1, k_width):
        y = y + xp[:, k:k + seq_len] * w[k]
    return y


def _pad_to_chunks(t):
    n_tok = t.shape[1] - N_META
    end = (-n_tok) % CHUNK
    widths = [(0, 0), (FRONT, end)] + [(0, 0)] * (t.ndim - 2)
    return jnp.pad(t, widths)


def ssd_mixer(z, xbc, dt_raw, conv_w, conv_b, dt_bias, A_log, D_skip, norm_w):
    f32 = jnp.float32
    bsz, seq_len, _ = z.shape
    hpg = SSD_HEADS // SSD_GROUPS
    xbc = jax.nn.silu(causal_dwconv(xbc, conv_w, conv_b))
    xs, b_in, c_in = jnp.split(xbc, [SSD_D_INNER, SSD_D_INNER + SSD_GROUPS * SSD_STATE], axis=-1)
    dt = jax.nn.softplus(dt_raw.astype(f32) + dt_bias.astype(f32))
    xs, b_in, c_in, dt = [_pad_to_chunks(t) for t in (xs, b_in, c_in, dt)]
    n_chunks = xs.shape[1] // CHUNK
    xs = xs.reshape(bsz, n_chunks, CHUNK, SSD_GROUPS, hpg, SSD_HEAD_DIM)
    b_in = b_in.reshape(bsz, n_chunks, CHUNK, SSD_GROUPS, SSD_STATE)
    c_in = c_in.reshape(bsz, n_chunks, CHUNK, SSD_GROUPS, SSD_STATE)
    dt = dt.reshape(bsz, n_chunks, CHUNK, SSD_GROUPS, hpg)
    A = -jnp.exp(A_log.astype(f32)).reshape(SSD_GROUPS, hpg)
    a_cs = jnp.cumsum(dt * A, axis=2)
    xdt = xs * dt[..., None]
    causal = jnp.tril(jnp.ones((CHUNK, CHUNK), dtype=bool))
    seg = a_cs[:, :, :, None] - a_cs[:, :, None, :]
    decay_ls = jnp.exp(jnp.where(causal[:, :, None, None], seg, -jnp.inf))
    cb = jnp.einsum('bclgn,bcsgn->bclsg', c_in, b_in)
    y = jnp.einsum('bclsgh,bcsghp->bclghp', cb[..., None] * decay_ls, xdt)
    decay_to_end = jnp.exp(a_cs[:, :, -1:] - a_cs)
    chunk_states = jnp.einsum('bcsgn,bcsghp->bcghpn', b_in, xdt * decay_to_end[..., None])
    chunk_decay = jnp.exp(a_cs[:, :, -1])

    def step(state, inp):
        st, dec = inp
        return state * dec[..., None, None] + st, state

    init = jnp.zeros_like(chunk_states[:, 0])
    _, prev = lax.scan(step, init, (jnp.moveaxis(chunk_states, 1, 0), jnp.moveaxis(chunk_decay, 1, 0)))
    prev = jnp.moveaxis(prev, 0, 1)
    y = y + jnp.einsum('bclgn,bcghpn->bclghp', c_in, prev) * jnp.exp(a_cs)[..., None]
    y = y + D_skip.astype(f32).reshape(SSD_GROUPS, hpg, 1) * xs
    y = y.reshape(bsz, n_chunks * CHUNK, SSD_D_INNER)[:, FRONT:FRONT + seq_len]
    yz = (y * jax.nn.silu(z.astype(f32))).reshape(bsz, seq_len, SSD_GROUPS, -1)
    yz = yz * lax.rsqrt(jnp.mean(yz * yz, axis=-1, keepdims=True) + EPS)
    return (yz.reshape(bsz, seq_len, SSD_D_INNER) * norm_w.astype(f32)).astype(z.dtype)


def _rotary(t, cos, sin):
    half = t.shape[-1] // 2
    t1, t2 = t[..., :half], t[..., half:]
    return jnp.concatenate([t1 * cos - t2 * sin, t2 * cos + t1 * sin], axis=-1).astype(t.dtype)


def retention_mixer(q, k, v, g):
    f32 = jnp.float32
    bsz, seq_len, _ = q.shape
    q = q.reshape(bsz, seq_len, RET_HEADS, RET_HEAD_QK)
    k = k.reshape(bsz, seq_len, RET_HEADS, RET_HEAD_QK)
    v = v.reshape(bsz, seq_len, RET_HEADS, RET_HEAD_V)
    pos = jnp.arange(seq_len, dtype=f32)
    inv_freq = ROPE_BASE ** (-jnp.linspace(0.0, 1.0, RET_HEAD_QK // 2, dtype=f32))
    ang = pos[:, None] * inv_freq[None, :]
    cos = jnp.cos(ang)[None, :, None, :]
    sin = jnp.sin(ang)[None, :, None, :]
    q = _rotary(q, cos, sin)
    k = _rotary(k, cos, sin) * (RET_HEAD_QK ** -0.5)
    q, k, v = [_pad_to_chunks(t) for t in (q, k, v)]
    n_chunks = q.shape[1] // CHUNK
    q = q.reshape(bsz, n_chunks, CHUNK, RET_HEADS, RET_HEAD_QK)
    k = k.reshape(bsz, n_chunks, CHUNK, RET_HEADS, RET_HEAD_QK)
    v = v.reshape(bsz, n_chunks, CHUNK, RET_HEADS, RET_HEAD_V)
    log_gamma = jnp.log(1.0 - 2.0 ** (-5.0 - jnp.arange(RET_HEADS, dtype=f32)))
    idx = jnp.arange(CHUNK, dtype=f32)
    causal = jnp.tril(jnp.ones((CHUNK, CHUNK), dtype=bool))
    dist = (idx[:, None] - idx[None, :])[..., None] * log_gamma
    decay_ls = jnp.exp(jnp.where(causal[..., None], dist, -jnp.inf)).transpose(2, 0, 1)
    scores = jnp.einsum('bclhd,bcshd->bchls', q, k) * decay_ls
    out = jnp.einsum('bchls,bcshe->bclhe', scores, v)
    k_dec = k * jnp.exp((CHUNK - 1.0 - idx)[:, None] * log_gamma)[..., None]
    kv = jnp.einsum('bcshd,bcshe->bchde', k_dec, v)
    chunk_decay = jnp.exp(CHUNK * log_gamma)

    def step(state, kv_c):
        return state * chunk_decay[:, None, None] + kv_c, state

    init = jnp.zeros_like(kv[:, 0])
    _, prev = lax.scan(step, init, jnp.moveaxis(kv, 1, 0))
    prev = jnp.moveaxis(prev, 0, 1)
    cross = jnp.einsum('bclhd,bchde->bclhe', q, prev) * jnp.exp((idx + 1.0)[:, None] * log_gamma)[..., None]
    out = (out + cross).reshape(bsz, n_chunks * CHUNK, RET_HEADS, RET_HEAD_V)[:, FRONT:FRONT + seq_len]
    out = out.astype(f32)
    out = out * lax.rsqrt(jnp.mean(out * out, axis=-1, keepdims=True) + EPS)
    return (jax.nn.silu(g.astype(f32)) * out.reshape(bsz, seq_len, RET_V_DIM)).astype(g.dtype)


def setup_inputs(seed: int = 0) -> dict:
    key = jax.random.key(seed)
    ks = jax.random.split(key, 24)
    f32 = jnp.float32
    nrm = lambda k, shape, scale: jax.random.normal(k, shape, f32) * scale
    dt0 = jnp.exp(jax.random.uniform(ks[6], (DEPTH, SSD_HEADS), f32, np.log(1e-3), np.log(1e-1)))
    return {
        'x': nrm(ks[0], (BATCH, SEQ, D_MODEL), 1.0),
        'meta_tokens': nrm(ks[1], (N_META, D_MODEL), 1.0),
        'mix_norm_w': 1.0 + nrm(ks[2], (DEPTH, D_MODEL), 0.01),
        'w_in': nrm(ks[3], (DEPTH, D_MODEL, IN_PROJ_DIM), D_MODEL ** -0.5),
        'ssd_conv_w': nrm(ks[4], (DEPTH, SSD_CONV, SSD_CONV_DIM), SSD_CONV ** -0.5),
        'ssd_conv_b': nrm(ks[5], (DEPTH, SSD_CONV_DIM), 0.01),
        'ssd_dt_bias': dt0 + jnp.log(-jnp.expm1(-dt0)),
        'ssd_A_log': jnp.log(jax.random.uniform(ks[7], (DEPTH, SSD_HEADS), f32, 1.0, 16.0)),
        'ssd_D': 1.0 + nrm(ks[8], (DEPTH, SSD_HEADS), 0.01),
        'ssd_norm_w': 1.0 + nrm(ks[9], (DEPTH, SSD_D_INNER), 0.01),
        'w_branch_ssd': nrm(ks[10], (DEPTH, SSD_D_INNER, D_MODEL), SSD_D_INNER ** -0.5),
        'w_branch_ret': nrm(ks[11], (DEPTH, RET_V_DIM, D_MODEL), RET_V_DIM ** -0.5),
        'w_out': nrm(ks[12], (DEPTH, D_MODEL, D_MODEL), D_MODEL ** -0.5),
        'ffn_norm_w': 1.0 + nrm(ks[13], (DEPTH, D_MODEL), 0.01),
        'w_up': nrm(ks[14], (DEPTH, D_MODEL, 2 * D_FF), D_MODEL ** -0.5),
        'ffn_conv_w': nrm(ks[15], (DEPTH, FFN_CONV, 2 * D_FF), FFN_CONV ** -0.5),
        'ffn_conv_b': nrm(ks[16], (DEPTH, 2 * D_FF), 0.01),
        'w_down': nrm(ks[17], (DEPTH, D_FF, D_MODEL), D_FF ** -0.5),
        'final_norm_w': 1.0 + nrm(ks[18], (D_MODEL,), 0.01),
    }


def reference(x, meta_tokens, mix_norm_w, w_in, ssd_conv_w, ssd_conv_b, ssd_dt_bias, ssd_A_log,
              ssd_D, ssd_norm_w, w_branch_ssd, w_branch_ret, w_out, ffn_norm_w, w_up,
              ffn_conv_w, ffn_conv_b, w_down, final_norm_w):
    bsz = x.shape[0]
    meta = jnp.broadcast_to(meta_tokens[None].astype(x.dtype), (bsz, N_META, D_MODEL))
    h = jnp.concatenate([meta, x], axis=1)
    for layer in range(DEPTH):
        u = rmsnorm(h, mix_norm_w[layer])
        proj = u @ w_in[layer]
        z, xbc, dt_raw, q, k, v, g, gate_ssd, gate_ret = jnp.split(proj, _split_points(), axis=-1)
        y_ssd = ssd_mixer(z, xbc, dt_raw, ssd_conv_w[layer], ssd_conv_b[layer], ssd_dt_bias[layer],
                          ssd_A_log[layer], ssd_D[layer], ssd_norm_w[layer])
        y_ret = retention_mixer(q, k, v, g)
        merged = (jax.nn.sigmoid(gate_ssd) * (y_ssd @ w_branch_ssd[layer])
                  + jax.nn.sigmoid(gate_ret) * (y_ret @ w_branch_ret[layer]))
        h = h + merged @ w_out[layer]
        u = rmsnorm(h, ffn_norm_w[layer])
        a = causal_dwconv(u @ w_up[layer], ffn_conv_w[layer], ffn_conv_b[layer])
        a_gate, a_val = jnp.split(a, 2, axis=-1)
        h = h + (jax.nn.silu(a_gate) * a_val) @ w_down[layer]
    out = rmsnorm(h, final_norm_w)
    return out[:, N_META:]
```

```python
import contextlib
import os
import numpy as np
import concourse.bass as bass
import concourse.mybir as mybir
from concourse.bass_utils import run_bass_kernel_spmd

F32 = mybir.dt.float32
BF16 = mybir.dt.bfloat16
ALU = mybir.AluOpType
AF = mybir.ActivationFunctionType
SAME_ENG_SYNC = True
KDMA = 8
SEMROT = 400
NSEM = {'pe': 20, 'act': 20, 'dve': 26, 'pool': 10, 'sp': 1}
EPS = 1e-6
D = 1024
NMETA = 16
CZ, CX, CB, CC, CDT, CQ, CK, CV, CG, CGS, CGR = 0, 2048, 4096, 4608, 5120, 5152, 6176, 7200, 9248, 11296, 12320
DFF = 2816


class Sched:
    ENGS = ['pe', 'act', 'dve', 'pool', 'sp']

    def __init__(self):
        self.ops = []
        self.last_w = {}
        self.readers = {}
        self.last_on = {e: None for e in self.ENGS}
        self.dma_ops = []

    def add(self, eng, fn, r=(), w=(), dma=False, extra_deps=()):
        idx = len(self.ops)
        deps = set(extra_deps)
        for k in r:
            if k in self.last_w:
                deps.add(self.last_w[k])
        for k in w:
            if k in self.last_w:
                deps.add(self.last_w[k])
            rd = self.readers.get(k)
            if rd:
                deps.update(rd[0].values())
                deps.update(rd[1])
        for k in r:
            rd = self.readers.setdefault(k, [{}, []])
            if dma:
                rd[1].append(idx)
            else:
                rd[0][eng] = idx
        for k in w:
            self.last_w[k] = idx
            self.readers[k] = [{}, []]
        deps.discard(idx)
        self.ops.append(dict(eng=eng, fn=fn, deps=deps, dma=dma))
        self.last_on[eng] = idx
        if dma:
            self.dma_ops.append(idx)
        return idx

    def barrier(self):
        lasts = [v for v in self.last_on.values() if v is not None]
        dmas = list(self.dma_ops)
        self.dma_ops = []
        for e in self.ENGS:
            self.add(e, None, extra_deps=set(lasts) | set(dmas))
        self.last_w = {}
        self.readers = {}

    def emit(self, nc, block, sems, dma_sems):
        ops = self.ops
        need = set()
        for i, o in enumerate(ops):
            for d in o['deps']:
                od = ops[d]
                if od['dma'] or od['fn'] is None:
                    continue
                if od['eng'] != o['eng'] or (SAME_ENG_SYNC and od['eng'] != 'pe'):
                    need.add(d)
        cnt = {e: 0 for e in self.ENGS}
        sigval = {}
        dcount = {e: 0 for e in self.ENGS}
        dinfo = {}
        for i, o in enumerate(ops):
            if o['dma']:
                e = o['eng']
                n = dcount[e]
                dcount[e] += 1
                dinfo[i] = (dma_sems[e][n % KDMA], 16 * (n // KDMA + 1), 16 * (n // KDMA), (e, n % KDMA))
            elif i in need:
                cnt[o['eng']] += 1
                c0 = cnt[o['eng']] - 1
                sigval[i] = (c0 // SEMROT, c0 % SEMROT + 1)
                assert c0 // SEMROT < NSEM[o['eng']], (o['eng'], c0)
        self.stats = dict(cnt=dict(cnt), dcount=dict(dcount), nops=len(ops))

        def run(engname, eng):
            waited = {}
            for i, o in enumerate(ops):
                if o['eng'] != engname:
                    continue
                for d in sorted(o['deps']):
                    od = ops[d]
                    if od['dma']:
                        s, v, _, key = dinfo[d]
                    else:
                        if od['fn'] is None:
                            continue
                        if od['eng'] == engname and (not SAME_ENG_SYNC or engname == 'pe'):
                            continue
                        si, v = sigval[d]
                        s = sems[od['eng']][si]
                        key = ('e', od['eng'], si)
                    if waited.get(key, 0) >= v:
                        continue
                    waited[key] = v
                    eng.wait_ge(s, v)
                if o['dma']:
                    s, v, prev, key = dinfo[i]
                    if prev > 0 and waited.get(key, 0) < prev:
                        eng.wait_ge(s, prev)
                        waited[key] = prev
                    o['fn'](eng).then_inc(s, 16)
                elif o['fn'] is not None:
                    ins = o['fn'](eng)
                    if i in sigval:
                        ins.then_inc(sems[engname][sigval[i][0]], 1)

        @block.tensor
        def _(e):
            run('pe', e)

        @block.scalar
        def _(e):
            run('act', e)

        @block.vector
        def _(e):
            run('dve', e)

        @block.gpsimd
        def _(e):
            run('pool', e)

        @block.sync
        def _(e):
            run('sp', e)


def build(NM, debug=False):
    NH = 3 * NM
    NMC = NM + 1
    NALL = NH + NMC
    L = NALL * 128
    T = NM * 128
    nc = bass.Bass("TRN2", target_bir_lowering=False)

    def din(name, shape, dt=F32):
        return nc.dram_tensor("c_" + name, shape, dt, kind="ExternalInput").ap()

    xT_d = din("xT", [D, L])
    cos_d = din("cosT", [128, L])
    sin_d = din("sinT", [128, L])
    maskc_d = din("maskc", [128, NALL])
    maskrow_d = din("maskrow", [128, 128])
    ident_d = din("ident", [128, 128])
    tri_d = din("tri", [128, 128])
    sut_d = din("sut", [128, 128])
    dmask_d = din("dmask", [128, 512])
    kd_d = din("kd", [128, 4])
    gl_d = din("gl", [128, 512])
    w1T_d = din("w1T", [128, 8])
    w2T_d = din("w2T", [128, 8])
    wfT_d = din("wfT", [128, 8])
    convw_d = din("convw", [128, 96])
    convb_d = din("convb", [128, 24])
    dtb_d = din("dtb", [1, 32])
    alog_d = din("alog", [1, 32])
    dsk_d = din("dsk", [1, 32])
    normw_d = din("normw", [1, 2048])
    fcw_d = din("fcw", [128, 132])
    fcb_d = din("fcb", [128, 44])
    w_in = din("w_in", [D, 13344])
    w_bs = din("w_bs", [2048, D])
    w_br = din("w_br", [2048, D])
    w_o = din("w_o", [D, D])
    w_up = din("w_up", [D, 2 * DFF])
    w_dn = din("w_dn", [DFF, D])
    outT_d = nc.dram_tensor("outT", [D, T], F32, kind="ExternalOutput").ap()
    yT_d = nc.dram_tensor("yT_scr", [NMC * 128, 32 * 128], BF16, kind="Internal").ap()
    h2T_d = nc.dram_tensor("h2T_scr", [D, NMC * 128], F32, kind="Internal").ap()
    dbg_d = nc.dram_tensor("dbg", [128, 4096], F32, kind="ExternalOutput").ap() if debug else None

    PH = os.environ.get('K_PHASES', 'HA,HB,M,D1,D2').split(',')
    S = Sched()
    gamma = [1.0 - 2.0 ** (-5.0 - h) for h in range(4)]
    cdec = [g ** 128 for g in gamma]

    def mm(out, lhsT, rhs, start, stop, r, w):
        S.add('pe', lambda e: e.matmul(out=out, lhsT=lhsT, rhs=rhs, start=start, stop=stop), r=r, w=w)

    def act(out, in_, func, r, w, bias=0.0, scale=1.0):
        S.add('act', lambda e: e.activation(out=out, in_=in_, func=func, bias=bias, scale=scale), r=r, w=w)

    def tt(eng, out, in0, in1, op, r, w):
        S.add(eng, lambda e: e.tensor_tensor(out=out, in0=in0, in1=in1, op=op), r=r, w=w)

    def ts(eng, out, in0, s1, s2, op0, op1, r, w):
        if s2 is None:
            S.add(eng, lambda e: e.tensor_scalar(out=out, in0=in0, scalar1=s1, scalar2=None, op0=op0), r=r, w=w)
        else:
            S.add(eng, lambda e: e.tensor_scalar(out=out, in0=in0, scalar1=s1, scalar2=s2, op0=op0, op1=op1), r=r, w=w)

    def stt(out, in0, scalar, in1, op0, op1, r, w):
        S.add('dve', lambda e: e.scalar_tensor_tensor(out=out, in0=in0, scalar=scalar, in1=in1, op0=op0, op1=op1), r=r, w=w)

    def cp(eng, out, in_, r, w):
        if eng == 'act':
            S.add('act', lambda e: e.copy(out=out, in_=in_), r=r, w=w)
        else:
            S.add(eng, lambda e: e.tensor_copy(out=out, in_=in_), r=r, w=w)

    def dma(eng, out, in_, r, w):
        S.add(eng, lambda e: e.dma_start(out=out, in_=in_), r=r, w=w, dma=True)

    def trn(out, in_, r, w):
        S.add('pe', lambda e: e.transpose(out=out, in_=in_, identity=identb[:]), r=list(r) + ['identb'], w=w)

    def recip(out, in_, r, w):
        S.add('dve', lambda e: e.reciprocal(out=out, in_=in_), r=r, w=w)

    def rsum(out, in_, r, w):
        S.add('dve', lambda e: e.reduce_sum(out=out, in_=in_, axis=mybir.AxisListType.X), r=r, w=w)

    with contextlib.ExitStack() as st0:
        A0 = st0.enter_context
        sems = {e: [A0(nc.semaphore(f"s_{e}_{i}")) for i in range(NSEM[e])] for e in Sched.ENGS}
        dma_sems = {e: [A0(nc.semaphore(f"d_{e}_{i}")) for i in range(KDMA)] for e in ['sp', 'pool']}
        dma_sems['pe'] = dma_sems['dve'] = dma_sems['act'] = []
        ps = [A0(nc.psum_tensor(f"ps{i}", [128, 512], F32)) for i in range(8)]
        psb = [p[:].bitcast(BF16) for p in ps]

        def sbuf(stack, name, shape, dt):
            return stack.enter_context(nc.sbuf_tensor(name, shape, dt))

        identf = sbuf(st0, "identf", [128, 128], F32)
        identb = sbuf(st0, "identb", [128, 128], BF16)
        tri = sbuf(st0, "tri", [128, 128], F32)
        sut = sbuf(st0, "sut", [128, 128], F32)
        onesf = sbuf(st0, "onesf", [128, 128], F32)
        onesb = sbuf(st0, "onesb", [128, 128], BF16)
        dmaskt = sbuf(st0, "dmask", [128, 512], F32)
        kdt = sbuf(st0, "kd", [128, 4], F32)
        glt = sbuf(st0, "gl", [128, 512], F32)
        w1T = sbuf(st0, "w1T", [128, 8], F32)
        w2T = sbuf(st0, "w2T", [128, 8], F32)
        wfT = sbuf(st0, "wfT", [128, 8], F32)
        convw = sbuf(st0, "convw", [128, 96], F32)
        convb = sbuf(st0, "convb", [128, 24], F32)
        dtb = sbuf(st0, "dtb", [128, 32], F32)
        aneg = sbuf(st0, "aneg", [128, 32], F32)
        dsk = sbuf(st0, "dsk", [128, 32], F32)
        fcw = sbuf(st0, "fcw", [128, 132], F32)
        fcb = sbuf(st0, "fcb", [128, 44], F32)
        maskc = sbuf(st0, "maskc", [128, NALL], F32)
        maskrow = sbuf(st0, "maskrow", [128, 128], F32)
        st_u = contextlib.ExitStack()
        uTm = sbuf(st_u, "uTm", [128, 8 * NMC * 128], BF16)
        st_state = contextlib.ExitStack()
        Sst = sbuf(st_state, "Sst", [128, 2048], F32)
        Rst = sbuf(st_state, "Rst", [128, 4096], F32)
        halo0 = sbuf(st_state, "halo0", [128, 24 * 3], BF16)
        normw = sbuf(st_state, "normw", [128, 2048], F32)

        for (t_, d_) in [(identf, ident_d), (tri, tri_d), (sut, sut_d), (dmaskt, dmask_d), (kdt, kd_d), (glt, gl_d),
                         (w1T, w1T_d), (w2T, w2T_d), (wfT, wfT_d), (convw, convw_d), (convb, convb_d), (fcw, fcw_d),
                         (fcb, fcb_d), (maskc, maskc_d), (maskrow, maskrow_d)]:
            dma('sp', t_[:], d_[:, :], r=[], w=[t_.name])
        for (t_, d_, n_) in [(dtb, dtb_d, 32), (aneg, alog_d, 32), (dsk, dsk_d, 32), (normw, normw_d, 2048)]:
            dma('sp', t_[:], d_[0:1, :].to_broadcast([128, n_]), r=[], w=[t_.name])
        cp('dve', identb[:], identf[:], r=['identf'], w=['identb'])
        S.add('dve', lambda e: e.memset(onesf[:], 1.0), w=['onesf'])
        S.add('dve', lambda e: e.memset(onesb[:], 1.0), w=['onesb'])
        S.add('dve', lambda e: e.memset(Sst[:], 0.0), w=['Sst'])
        S.add('pool', lambda e: e.memset(Rst[:], 0.0), w=['Rst'])
        for _i in range(int(os.environ.get('K_DUMMY_ACT', '0'))):
            cp('act', onesb[:, 0:8], onesf[:, 0:8], r=['onesf'], w=['onesb'])
        for _i in range(int(os.environ.get('K_DUMMY_DVE', '0'))):
            cp('dve', onesb[:, 0:8], onesf[:, 0:8], r=['onesf'], w=['onesb'])
        act(aneg[:], aneg[:], AF.Exp, r=['aneg'], w=['aneg'])
        ts('dve', aneg[:], aneg[:], -1.0, None, ALU.mult, None, r=['aneg'], w=['aneg'])

        def load_w(eng, dst, dst_c0, src, r0, c0, ncols, key):
            c = 0
            while c < ncols:
                n = min(2048, ncols - c)
                dma('pool', dst[:, dst_c0 + c:dst_c0 + c + n], src[r0:r0 + 128, c0 + c:c0 + c + n], r=[], w=[key])
                c += n

        def rms_block(stack_bufs, src, ntok, wT, dst_bf, key_src, key_dst, psbank, mask_bc=None, dst_scale_f32=None):
            sqb, rstd = stack_bufs
            act(sqb[:, 0:8 * ntok], src[:, 0:8 * ntok], AF.Square, r=[key_src], w=[sqb.name])
            for kt in range(8):
                mm(ps[psbank][:, 0:ntok], onesb[:], sqb[:, kt * ntok:(kt + 1) * ntok], kt == 0, kt == 7,
                   r=[sqb.name, 'onesb'], w=[f'ps{psbank}'])
            act(rstd[:, 0:ntok], ps[psbank][:, 0:ntok], AF.Sqrt, r=[f'ps{psbank}'], w=[rstd.name], bias=EPS, scale=1.0 / D)
            recip(rstd[:, 0:ntok], rstd[:, 0:ntok], r=[rstd.name], w=[rstd.name])
            if mask_bc is not None:
                tt('dve', rstd[:, 0:ntok], rstd[:, 0:ntok], mask_bc, ALU.mult, r=[rstd.name, 'maskrow'], w=[rstd.name])
            for kt in range(8):
                stt(dst_bf[kt], src[:, kt * ntok:(kt + 1) * ntok], wT[:, kt:kt + 1], rstd[:, 0:ntok], ALU.mult, ALU.mult,
                    r=[key_src, rstd.name, wT.name], w=[key_dst])

        def dt_chain(bufs, ps_dt, nh, hoff, mcol, keyp):
            xdt_, ax, ex, dtv, av, acs, atot, dte, cd, ea, dtd = bufs
            tt('dve', xdt_[:, 0:nh], ps_dt, dtb[:, hoff:hoff + nh], ALU.add, r=[keyp, 'dtb'], w=[xdt_.name])
            ts('dve', ax[:, 0:nh], xdt_[:, 0:nh], -1.0, None, ALU.mult, None, r=[xdt_.name], w=[ax.name])
            tt('dve', ax[:, 0:nh], ax[:, 0:nh], xdt_[:, 0:nh], ALU.max, r=[xdt_.name, ax.name], w=[ax.name])
            act(ex[:, 0:nh], ax[:, 0:nh], AF.Exp, r=[ax.name], w=[ex.name], scale=-1.0)
            act(ex[:, 0:nh], ex[:, 0:nh], AF.Ln, r=[ex.name], w=[ex.name], bias=1.0)
            ts('dve', xdt_[:, 0:nh], xdt_[:, 0:nh], 0.0, None, ALU.max, None, r=[xdt_.name], w=[xdt_.name])
            tt('dve', dtv[:, 0:nh], xdt_[:, 0:nh], ex[:, 0:nh], ALU.add, r=[xdt_.name, ex.name], w=[dtv.name])
            if mcol is not None:
                ts('dve', dtv[:, 0:nh], dtv[:, 0:nh], mcol, None, ALU.mult, None, r=[dtv.name, 'maskc'], w=[dtv.name])
            tt('dve', av[:, 0:nh], dtv[:, 0:nh], aneg[:, hoff:hoff + nh], ALU.mult, r=[dtv.name, 'aneg'], w=[av.name])
            return dtv, av

        BLK = 256
        nblk = NH * 128 // BLK
        with contextlib.ExitStack() as st:
            WH = 2560 + 32
            wH = sbuf(st, "wH", [128, 8 * WH], BF16)
            wC = sbuf(st, "wC", [128, 8 * 512], BF16)
            diag = sbuf(st, "diag", [128, 96 * 128], BF16)
            xblk = sbuf(st, "xblk", [128, 8 * BLK], F32)
            sqb = sbuf(st, "sqb", [128, 8 * BLK], BF16)
            rstd = sbuf(st, "rstd", [128, BLK], F32)
            uT = sbuf(st, "uT", [128, 8 * BLK], BF16)
            xpre = sbuf(st, "xpre", [128, 24 * (BLK + 3)], BF16)
            hsave = sbuf(st, "hsave", [128, 24 * 3], BF16)
            xcv = sbuf(st, "xcv", [128, 20 * BLK], BF16)
            xdd = sbuf(st, "xdd", [128, 2048], BF16)
            btm = sbuf(st, "btm", [128, 512], BF16)
            dtbufs = [sbuf(st, f"dtb{i}", [128, 32], F32) for i in range(11)]
            stmp = sbuf(st, "stmp", [128, 512], F32)
            for i in range(96):
                ts('dve', diag[:, i * 128:(i + 1) * 128], identf[:], convw[:, i:i + 1], None, ALU.mult, None,
                   r=['identf', 'convw'], w=['diag'])
            for kt in range(8):
                load_w('pool', wH, kt * WH, w_in, kt * 128, CX, 2560, 'wH')
                load_w('pool', wH, kt * WH + 2560, w_in, kt * 128, CDT, 32, 'wH')
                load_w('pool', wC, kt * 512, w_in, kt * 128, CC, 512, 'wC')
            S.add('dve', lambda e: e.memset(xpre[:], 0.0), w=['xpre'])
            S.add('dve', lambda e: e.memset(hsave[:], 0.0), w=['hsave'])
            xpv = xpre[:, :].rearrange("p (t c) -> p t c", t=24)
            for b in range(nblk if 'HA' in PH else 0):
                t0 = b * BLK
                last = (b == nblk - 1)
                dma('sp', xblk[:, :].rearrange("p (kt t) -> p kt t", kt=8),
                    xT_d[:, t0:t0 + BLK].rearrange("(kt p) t -> p kt t", p=128), r=[], w=['xblk'])
                rms_block((sqb, rstd), xblk, BLK, w1T, [uT[:, kt * BLK:(kt + 1) * BLK] for kt in range(8)], 'xblk', 'uT', 0)
                cp('pool', xpv[:, :, 0:3], hsave[:, :].rearrange("p (t c) -> p t c", t=24), r=['hsave'], w=['xpre'])
                ntile = 24 if last else 20
                for t in range(ntile):
                    bank = 1 + (t % 2)
                    for kt in range(8):
                        lw = wH[:, kt * WH + t * 128:kt * WH + (t + 1) * 128] if t < 20 else wC[:, kt * 512 + (t - 20) * 128:kt * 512 + (t - 19) * 128]
                        mm(ps[bank][:, 0:BLK], lw, uT[:, kt * BLK:(kt + 1) * BLK], kt == 0, kt == 7, r=['wH', 'wC', 'uT'], w=[f'ps{bank}'])
                    cp('act', xpre[:, t * (BLK + 3) + 3:t * (BLK + 3) + BLK + 3], ps[bank][:, 0:BLK], r=[f'ps{bank}'], w=['xpre'])
                cp('pool', hsave[:, :].rearrange("p (t c) -> p t c", t=24), xpv[:, :, BLK:BLK + 3], r=['xpre'], w=['hsave'])
                if last:
                    cp('pool', halo0[:, :], hsave[:, :], r=['hsave'], w=['halo0'])
                for t in range(20):
                    bank = 3 + (t % 2)
                    for k in range(4):
                        mm(ps[bank][:, 0:BLK], diag[:, (t * 4 + k) * 128:(t * 4 + k + 1) * 128], xpre[:, t * (BLK + 3) + k:t * (BLK + 3) + k + BLK],
                           k == 0, k == 3, r=['diag', 'xpre'], w=[f'ps{bank}'])
                    act(xcv[:, t * BLK:(t + 1) * BLK], ps[bank][:, 0:BLK], AF.Silu, r=[f'ps{bank}', 'convb'], w=['xcv'],
                        bias=convb[:, t:t + 1])
                for j in range(BLK // 128):
                    c = b * (BLK // 128) + j
                    for kt in range(8):
                        mm(ps[5][:, 0:32], uT[:, kt * BLK + j * 128:kt * BLK + (j + 1) * 128], wH[:, kt * WH + 2560:kt * WH + 2592],
                           kt == 0, kt == 7, r=['uT', 'wH'], w=['ps5'])
                    dtv, av = dt_chain(dtbufs, ps[5][:, 0:32], 32, 0, maskc[:, c:c + 1], 'ps5')
                    acs, atot, dte, cd, dtd = dtbufs[5], dtbufs[6], dtbufs[7], dtbufs[8], dtbufs[10]
                    mm(ps[5][:, 64:96], tri[:], av[:, 0:32], True, True, r=['tri', av.name], w=['ps5'])
                    mm(ps[5][:, 128:160], onesf[:], av[:, 0:32], True, True, r=['onesf', av.name], w=['ps5'])
                    cp('dve', atot[:, 0:32], ps[5][:, 128:160], r=['ps5'], w=[atot.name])
                    tt('dve', dte[:, 0:32], atot[:, 0:32], ps[5][:, 64:96], ALU.subtract, r=['ps5', atot.name], w=[dte.name])
                    act(dte[:, 0:32], dte[:, 0:32], AF.Exp, r=[dte.name], w=[dte.name])
                    act(cd[:, 0:32], atot[:, 0:32], AF.Exp, r=[atot.name], w=[cd.name])
                    tt('dve', dtd[:, 0:32], dtv[:, 0:32], dte[:, 0:32], ALU.mult, r=[dtv.name, dte.name], w=[dtd.name])
                    for half in range(2):
                        bank = 1 + half
                        for t8 in range(8):
                            t = half * 8 + t8
                            trn(psb[bank][:, t8 * 128:(t8 + 1) * 128], xcv[:, t * BLK + j * 128:t * BLK + (j + 1) * 128],
                                r=['xcv'], w=[f'ps{bank}'])
                        tt('dve', xdd[:, half * 1024:(half + 1) * 1024].rearrange("p (h d) -> p h d", h=16),
                           psb[bank][:, 0:1024].rearrange("p (h d) -> p h d", h=16),
                           dtd[:, half * 16:(half + 1) * 16].unsqueeze(2).to_broadcast([128, 16, 64]), ALU.mult,
                           r=[f'ps{bank}', dtd.name], w=['xdd'])
                    for t4 in range(4):
                        trn(psb[6][:, t4 * 128:(t4 + 1) * 128], xcv[:, (16 + t4) * BLK + j * 128:(16 + t4) * BLK + (j + 1) * 128],
                            r=['xcv'], w=['ps6'])
                    cp('act', btm[:, :], psb[6][:, 0:512], r=['ps6'], w=['btm'])
                    for g in range(4):
                        bank = 3 + (g % 2)
                        mm(ps[bank][:, :], btm[:, g * 128:(g + 1) * 128], xdd[:, g * 512:(g + 1) * 512], True, True,
                           r=['btm', 'xdd'], w=[f'ps{bank}'])
                        tt('pool', stmp[:, :].rearrange("p (h d) -> p h d", h=8),
                           Sst[:, g * 512:(g + 1) * 512].rearrange("p (h d) -> p h d", h=8),
                           cd[:, g * 8:(g + 1) * 8].unsqueeze(2).to_broadcast([128, 8, 64]), ALU.mult,
                           r=['Sst', cd.name], w=['stmp'])
                        tt('dve', Sst[:, g * 512:(g + 1) * 512], ps[bank][:, :], stmp[:, :], ALU.add,
                           r=[f'ps{bank}', 'stmp'], w=['Sst'])
            S.barrier()

        with contextlib.ExitStack() as st:
            WH = 3072
            wH = sbuf(st, "wHb", [128, 8 * WH], BF16)
            xblk = sbuf(st, "xblkb", [128, 8 * 512], F32)
            sqb = sbuf(st, "sqbb", [128, 8 * 512], BF16)
            rstd = sbuf(st, "rstdb", [128, 512], F32)
            uT = sbuf(st, "uTb", [128, 8 * 512], BF16)
            cosb = sbuf(st, "cosb", [128, 512], F32)
            sinb = sbuf(st, "sinb", [128, 512], F32)
            rt = [sbuf(st, f"rt{i}", [128, 512], F32) for i in range(2)]
            krT = sbuf(st, "krT", [128, 8 * 512], BF16)
            vbf = sbuf(st, "vbf", [128, 2048], BF16)
            kdec = sbuf(st, "kdec", [128, 1024], BF16)
            kdm = sbuf(st, "kdm", [128, 4], F32)
            for kt in range(8):
                load_w('pool', wH, kt * WH, w_in, kt * 128, CK, 3072, 'wHb')
            for b in range(NH // 4 if 'HB' in PH else 0):
                t0 = b * 512
                dma('sp', xblk[:, :].rearrange("p (kt t) -> p kt t", kt=8),
                    xT_d[:, t0:t0 + 512].rearrange("(kt p) t -> p kt t", p=128), r=[], w=['xblkb'])
                dma('sp', cosb[:], cos_d[:, t0:t0 + 512], r=[], w=['cosb'])
                dma('sp', sinb[:], sin_d[:, t0:t0 + 512], r=[], w=['sinb'])
                rms_block((sqb, rstd), xblk, 512, w1T, [uT[:, kt * 512:(kt + 1) * 512] for kt in range(8)], 'xblkb', 'uTb', 0)
                for h in range(4):
                    for half in range(2):
                        bank = 1 + half
                        t = 2 * h + half
                        for kt in range(8):
                            mm(ps[bank][:, :], wH[:, kt * WH + t * 128:kt * WH + (t + 1) * 128],
                               uT[:, kt * 512:(kt + 1) * 512], kt == 0, kt == 7, r=['wHb', 'uTb'], w=[f'ps{bank}'])
                    k1, k2 = ps[1][:, :], ps[2][:, :]
                    tt('dve', rt[0][:], k1, cosb[:], ALU.mult, r=['ps1', 'cosb'], w=['rt0'])
                    tt('dve', rt[1][:], k2, sinb[:], ALU.mult, r=['ps2', 'sinb'], w=['rt1'])
                    tt('pool', krT[:, (2 * h) * 512:(2 * h + 1) * 512], rt[0][:], rt[1][:], ALU.subtract, r=['rt0', 'rt1'], w=['krT'])
                    tt('dve', rt[0][:], k2, cosb[:], ALU.mult, r=['ps2', 'cosb'], w=['rt0'])
                    tt('dve', rt[1][:], k1, sinb[:], ALU.mult, r=['ps1', 'sinb'], w=['rt1'])
                    tt('pool', krT[:, (2 * h + 1) * 512:(2 * h + 2) * 512], rt[0][:], rt[1][:], ALU.add, r=['rt0', 'rt1'], w=['krT'])
                for j in range(4):
                    c = b * 4 + j
                    for n4 in range(4):
                        bank = 3 + (n4 % 2)
                        for kt in range(8):
                            mm(ps[bank][:, :], uT[:, kt * 512 + j * 128:kt * 512 + (j + 1) * 128],
                               wH[:, kt * WH + 1024 + n4 * 512:kt * WH + 1024 + (n4 + 1) * 512], kt == 0, kt == 7,
                               r=['uTb', 'wHb'], w=[f'ps{bank}'])
                        cp('act', vbf[:, n4 * 512:(n4 + 1) * 512], ps[bank][:, :], r=[f'ps{bank}'], w=['vbf'])
                    for t8 in range(8):
                        trn(psb[5][:, t8 * 128:(t8 + 1) * 128], krT[:, t8 * 512 + j * 128:t8 * 512 + (j + 1) * 128],
                            r=['krT'], w=['ps5'])
                    ts('dve', kdm[:, :], kdt[:, :], maskc[:, c:c + 1], None, ALU.mult, None, r=['kd', 'maskc'], w=['kdm'])
                    for h in range(4):
                        act(kdec[:, h * 256:(h + 1) * 256], psb[5][:, h * 256:(h + 1) * 256], AF.Copy, r=['ps5', 'kdm'], w=['kdec'],
                            scale=kdm[:, h:h + 1])
                    for h in range(4):
                        for dtile in range(2):
                            bank = 6 + dtile
                            mm(ps[bank][:, :], kdec[:, h * 256 + dtile * 128:h * 256 + (dtile + 1) * 128],
                               vbf[:, h * 512:(h + 1) * 512], True, True, r=['kdec', 'vbf'], w=[f'ps{bank}'])
                            ro = (h * 2 + dtile) * 512
                            stt(Rst[:, ro:ro + 512], Rst[:, ro:ro + 512], float(cdec[h]), ps[bank][:, :], ALU.mult, ALU.add,
                                r=['Rst', f'ps{bank}'], w=['Rst'])
            S.barrier()

        if debug:
            dma('sp', dbg_d[:, 0:2048], Sst[:], r=['Sst'], w=[])
            dma('sp', dbg_d[:, 2048:4096], Rst[:, 0:2048], r=['Rst'], w=[])

        if True:
            st = contextlib.ExitStack()
            W = NMC * 128
            diag = sbuf(st, "diagm", [128, 96 * 128], BF16)
            for i in range(96):
                ts('dve', diag[:, i * 128:(i + 1) * 128], identf[:], convw[:, i:i + 1], None, ALU.mult, None,
                   r=['identf', 'convw'], w=['diagm'])
            with contextlib.ExitStack() as st2:
                xc = [sbuf(st2, f"xc{i}", [128, 8 * 128], F32) for i in range(2)]
                sqb = sbuf(st2, "sqbm", [128, 8 * 128], BF16)
                rstd = sbuf(st2, "rstdm", [128, 128], F32)
                for c in range(NMC):
                    t0 = (NH + c) * 128
                    xb = xc[c % 2]
                    dma('sp', xb[:, :].rearrange("p (kt t) -> p kt t", kt=8),
                        xT_d[:, t0:t0 + 128].rearrange("(kt p) t -> p kt t", p=128), r=[], w=[xb.name])
                    rms_block((sqb, rstd), xb, 128, w1T, [uTm[:, kt * W + c * 128:kt * W + (c + 1) * 128] for kt in range(8)],
                              xb.name, 'uTm', 0)
                S.barrier()

            WG = 1536
            wg = [sbuf(st, f"wg{i}", [128, 8 * WG], BF16) for i in range(2)]
            Sbf = sbuf(st, "Sbf", [128, 512], BF16)
            Rbf = sbuf(st, "Rbf", [128, 1024], BF16)
            xpm = [sbuf(st, f"xpm{i}", [128, 6 * 131], BF16) for i in range(2)]
            xcm = sbuf(st, "xcm", [128, 6 * 128], BF16)
            dtbufs = [sbuf(st, f"dtm{i}", [128, 32], F32) for i in range(11)]
            eam = sbuf(st, "eam", [128, 8], F32)
            xdt = sbuf(st, "xdt", [128, 512], BF16)
            xsD = sbuf(st, "xsD", [128, 512], F32)
            xddm = sbuf(st, "xddm", [128, 512], BF16)
            btmm = sbuf(st, "btmm", [128, 128], BF16)
            cbt = sbuf(st, "cbt", [128, 128], F32)
            rhsA = sbuf(st, "rhsA", [128, 1024], F32)
            decT = sbuf(st, "decT", [128, 1024], F32)
            Gm = sbuf(st, "Gm", [128, 1024], BF16)
            t1 = sbuf(st, "t1", [128, 512], F32)
            t2 = sbuf(st, "t2", [128, 512], F32)
            ybuf = sbuf(st, "ybuf", [128, 512], F32)
            szb = sbuf(st, "szb", [128, 512], F32)
            sq2 = sbuf(st, "sq2", [128, 512], F32)
            ss = sbuf(st, "ss", [128, 2], F32)
            ynb = sbuf(st, "ynb", [128, 512], BF16)
            yTs = [sbuf(st, f"yTs{i}", [128, 512], BF16) for i in range(2)]
            stmp = sbuf(st, "stmpm", [128, 512], F32)
            cosc = sbuf(st, "cosc", [128, 128], F32)
            sinc = sbuf(st, "sinc", [128, 128], F32)
            cosq = sbuf(st, "cosq", [128, 128], F32)
            sinq = sbuf(st, "sinq", [128, 128], F32)
            rtm = [sbuf(st, f"rtm{i}", [128, 128], F32) for i in range(2)]
            qsT = sbuf(st, "qsT", [128, 256], BF16)
            krm = sbuf(st, "krm", [128, 256], BF16)
            PT = sbuf(st, "PT", [128, 128], BF16)
            vbm = sbuf(st, "vbm", [128, 512], BF16)
            sgm = sbuf(st, "sgm", [128, 512], F32)
            kdm = sbuf(st, "kdmm", [128, 1], F32)
            kdecm = sbuf(st, "kdecm", [128, 256], BF16)

            def load_group(gi):
                wt = wg[gi % 2]
                key = wt.name
                for kt in range(8):
                    base = kt * WG
                    if gi < 4:
                        g = gi
                        load_w('pool', wt, base + 0, w_in, kt * 128, CZ + g * 512, 512, key)
                        load_w('pool', wt, base + 512, w_in, kt * 128, CX + g * 512, 512, key)
                        load_w('pool', wt, base + 1024, w_in, kt * 128, CB + g * 128, 128, key)
                        load_w('pool', wt, base + 1152, w_in, kt * 128, CC + g * 128, 128, key)
                        load_w('pool', wt, base + 1280, w_in, kt * 128, CDT + g * 8, 8, key)
                    else:
                        h = gi - 4
                        load_w('pool', wt, base + 0, w_in, kt * 128, CQ + h * 256, 256, key)
                        load_w('pool', wt, base + 256, w_in, kt * 128, CK + h * 256, 256, key)
                        load_w('pool', wt, base + 512, w_in, kt * 128, CV + h * 512, 512, key)
                        load_w('pool', wt, base + 1024, w_in, kt * 128, CG + h * 512, 512, key)

            load_group(0)
            GSEL = [int(v) for v in os.environ.get('K_GROUPS', '0,1,2,3,4,5,6,7').split(',')]
            for gi in (range(8) if 'M' in PH else []):
                if gi not in GSEL:
                    continue
                if gi + 1 < 8:
                    load_group(gi + 1)
                wt = wg[gi % 2]
                wk = wt.name
                if gi < 4:
                    g = gi
                    cp('act', Sbf[:, :], Sst[:, g * 512:(g + 1) * 512], r=['Sst'], w=['Sbf'])
                    for lt, gt in enumerate([4 * g, 4 * g + 1, 4 * g + 2, 4 * g + 3, 16 + g, 20 + g]):
                        cp('pool', xpm[1][:, lt * 131 + 128:lt * 131 + 131], halo0[:, gt * 3:gt * 3 + 3], r=['halo0'], w=['xpm1'])
                    for c in range(NMC):
                        ca = NH + c
                        xp = xpm[c % 2]
                        xpo = xpm[(c + 1) % 2]
                        us = lambda kt: uTm[:, kt * W + c * 128:kt * W + (c + 1) * 128]
                        cp('pool', xp[:, :].rearrange("p (t c) -> p t c", t=6)[:, :, 0:3],
                           xpo[:, :].rearrange("p (t c) -> p t c", t=6)[:, :, 128:131], r=[xpo.name], w=[xp.name])
                        for lt in range(6):
                            co = 512 + lt * 128
                            for kt in range(8):
                                mm(ps[0][:, (lt % 4) * 128:(lt % 4 + 1) * 128] if lt < 4 else ps[1][:, (lt - 4) * 128:(lt - 3) * 128],
                                   wt[:, kt * WG + co:kt * WG + co + 128], us(kt), kt == 0, kt == 7, r=[wk, 'uTm'],
                                   w=['ps0' if lt < 4 else 'ps1'])
                        for lt in range(6):
                            src = ps[0][:, lt * 128:(lt + 1) * 128] if lt < 4 else ps[1][:, (lt - 4) * 128:(lt - 3) * 128]
                            cp('act', xp[:, lt * 131 + 3:lt * 131 + 131], src, r=['ps0' if lt < 4 else 'ps1'], w=[xp.name])
                        gts = [4 * g, 4 * g + 1, 4 * g + 2, 4 * g + 3, 16 + g, 20 + g]
                        for lt in range(6):
                            gt = gts[lt]
                            dst = ps[2][:, (lt % 4) * 128:(lt % 4 + 1) * 128] if lt < 4 else ps[3][:, (lt - 4) * 128:(lt - 3) * 128]
                            for k in range(4):
                                mm(dst, diag[:, (gt * 4 + k) * 128:(gt * 4 + k + 1) * 128], xp[:, lt * 131 + k:lt * 131 + k + 128],
                                   k == 0, k == 3, r=['diagm', xp.name], w=['ps2' if lt < 4 else 'ps3'])
                        for lt in range(6):
                            gt = gts[lt]
                            src = ps[2][:, lt * 128:(lt + 1) * 128] if lt < 4 else ps[3][:, (lt - 4) * 128:(lt - 3) * 128]
                            act(xcm[:, lt * 128:(lt + 1) * 128], src, AF.Silu, r=['ps2' if lt < 4 else 'ps3', 'convb'], w=['xcm'],
                                bias=convb[:, gt:gt + 1])
                        BT = xcm[:, 512:640]
                        CT = xcm[:, 640:768]
                        for kt in range(8):
                            mm(ps[4][:, :], us(kt), wt[:, kt * WG:kt * WG + 512], kt == 0, kt == 7, r=['uTm', wk], w=['ps4'])
                        for kt in range(8):
                            mm(ps[5][:, 0:8], us(kt), wt[:, kt * WG + 1280:kt * WG + 1288], kt == 0, kt == 7, r=['uTm', wk], w=['ps5'])
                        mcol = maskc[:, ca:ca + 1] if c == 0 else None
                        dtv, av = dt_chain(dtbufs, ps[5][:, 0:8], 8, g * 8, mcol, 'ps5')
                        acs, atot, dte, cd, dtd = dtbufs[5], dtbufs[6], dtbufs[7], dtbufs[8], dtbufs[10]
                        mm(ps[5][:, 64:72], tri[:], av[:, 0:8], True, True, r=['tri', av.name], w=['ps5'])
                        mm(ps[5][:, 128:136], onesf[:], av[:, 0:8], True, True, r=['onesf', av.name], w=['ps5'])
                        cp('dve', atot[:, 0:8], ps[5][:, 128:136], r=['ps5'], w=[atot.name])
                        tt('dve', dte[:, 0:8], atot[:, 0:8], ps[5][:, 64:72], ALU.subtract, r=['ps5', atot.name], w=[dte.name])
                        act(dte[:, 0:8], dte[:, 0:8], AF.Exp, r=[dte.name], w=[dte.name])
                        act(cd[:, 0:8], atot[:, 0:8], AF.Exp, r=[atot.name], w=[cd.name])
                        act(eam[:, 0:8], ps[5][:, 64:72], AF.Exp, r=['ps5'], w=['eam'])
                        tt('dve', dtd[:, 0:8], dtv[:, 0:8], dte[:, 0:8], ALU.mult, r=[dtv.name, dte.name], w=[dtd.name])
                        for lt in range(5):
                            trn(psb[6][:, lt * 128:(lt + 1) * 128], xcm[:, lt * 128:(lt + 1) * 128], r=['xcm'], w=['ps6'])
                        xs_ps = psb[6][:, 0:512].rearrange("p (h d) -> p h d", h=8)
                        tt('dve', xdt[:, :].rearrange("p (h d) -> p h d", h=8), xs_ps,
                           dtv[:, 0:8].unsqueeze(2).to_broadcast([128, 8, 64]), ALU.mult, r=['ps6', dtv.name], w=['xdt'])
                        tt('dve', xddm[:, :].rearrange("p (h d) -> p h d", h=8), xs_ps,
                           dtd[:, 0:8].unsqueeze(2).to_broadcast([128, 8, 64]), ALU.mult, r=['ps6', dtd.name], w=['xddm'])
                        tt('dve', xsD[:, :].rearrange("p (h d) -> p h d", h=8), xs_ps,
                           dsk[:, g * 8:(g + 1) * 8].unsqueeze(2).to_broadcast([128, 8, 64]), ALU.mult, r=['ps6', 'dsk'], w=['xsD'])
                        cp('act', btmm[:, :], psb[6][:, 512:640], r=['ps6'], w=['btmm'])
                        mm(ps[7][:, 0:128], BT, CT, True, True, r=['xcm'], w=['ps7'])
                        tt('dve', cbt[:, :], ps[7][:, 0:128], tri[:], ALU.mult, r=['ps7', 'tri'], w=['cbt'])
                        tt('pool', rhsA[:, :].rearrange("p (h l) -> p h l", h=8),
                           av[:, 0:8].unsqueeze(2).to_broadcast([128, 8, 128]),
                           tri[:, :].unsqueeze(1).to_broadcast([128, 8, 128]), ALU.mult, r=[av.name, 'tri'], w=['rhsA'])
                        mm(ps[0][:, :], sut[:], rhsA[:, 0:512], True, True, r=['sut', 'rhsA'], w=['ps0'])
                        mm(ps[1][:, :], sut[:], rhsA[:, 512:1024], True, True, r=['sut', 'rhsA'], w=['ps1'])
                        act(decT[:, 0:512], ps[0][:, :], AF.Exp, r=['ps0'], w=['decT'])
                        act(decT[:, 512:1024], ps[1][:, :], AF.Exp, r=['ps1'], w=['decT'])
                        tt('dve', Gm[:, :].rearrange("p (h l) -> p h l", h=8), decT[:, :].rearrange("p (h l) -> p h l", h=8),
                           cbt[:, :].unsqueeze(1).to_broadcast([128, 8, 128]), ALU.mult, r=['decT', 'cbt'], w=['Gm'])
                        for hh in range(8):
                            mm(ps[2][:, hh * 64:(hh + 1) * 64], Gm[:, hh * 128:(hh + 1) * 128], xdt[:, hh * 64:(hh + 1) * 64],
                               True, True, r=['Gm', 'xdt'], w=['ps2'])
                        mm(ps[3][:, :], CT, Sbf[:, :], True, True, r=['xcm', 'Sbf'], w=['ps3'])
                        tt('dve', t1[:, :].rearrange("p (h d) -> p h d", h=8), ps[3][:, :].rearrange("p (h d) -> p h d", h=8),
                           eam[:, 0:8].unsqueeze(2).to_broadcast([128, 8, 64]), ALU.mult, r=['ps3', 'eam'], w=['t1'])
                        tt('pool', t2[:, :], t1[:, :], xsD[:, :], ALU.add, r=['t1', 'xsD'], w=['t2'])
                        tt('dve', ybuf[:, :], ps[2][:, :], t2[:, :], ALU.add, r=['ps2', 't2'], w=['ybuf'])
                        mm(ps[7][:, :], btmm[:, :], xddm[:, :], True, True, r=['btmm', 'xddm'], w=['ps7'])
                        tt('pool', stmp[:, :].rearrange("p (h d) -> p h d", h=8),
                           Sst[:, g * 512:(g + 1) * 512].rearrange("p (h d) -> p h d", h=8),
                           cd[:, 0:8].unsqueeze(2).to_broadcast([128, 8, 64]), ALU.mult, r=['Sst', cd.name], w=['stmp'])
                        tt('dve', Sst[:, g * 512:(g + 1) * 512], ps[7][:, :], stmp[:, :], ALU.add, r=['ps7', 'stmp'], w=['Sst'])
                        cp('act', Sbf[:, :], Sst[:, g * 512:(g + 1) * 512], r=['Sst'], w=['Sbf'])
                        act(szb[:, :], ps[4][:, :], AF.Silu, r=['ps4'], w=['szb'])
                        tt('pool', ybuf[:, :], ybuf[:, :], szb[:, :], ALU.mult, r=['ybuf', 'szb'], w=['ybuf'])
                        act(sq2[:, :], ybuf[:, :], AF.Square, r=['ybuf'], w=['sq2'])
                        rsum(ss[:, 0:1], sq2[:, :], r=['sq2'], w=['ss'])
                        act(ss[:, 0:1], ss[:, 0:1], AF.Sqrt, r=['ss'], w=['ss'], bias=EPS, scale=1.0 / 512)
                        recip(ss[:, 0:1], ss[:, 0:1], r=['ss'], w=['ss'])
                        stt(ynb[:, :], ybuf[:, :], ss[:, 0:1], normw[:, g * 512:(g + 1) * 512], ALU.mult, ALU.mult,
                            r=['ybuf', 'ss', 'normw'], w=['ynb'])
                        yT = yTs[c % 2]
                        for lt in range(4):
                            trn(psb[6][:, lt * 128:(lt + 1) * 128], ynb[:, lt * 128:(lt + 1) * 128], r=['ynb'], w=['ps6'])
                        cp('act', yT[:, :], psb[6][:, 0:512], r=['ps6'], w=[yT.name])
                        dma('sp', yT_d[c * 128:(c + 1) * 128, (4 * g) * 128:(4 * g + 4) * 128], yT[:, :], r=[yT.name], w=[])
                else:
                    h = gi - 4
                    cp('act', Rbf[:, :], Rst[:, h * 1024:(h + 1) * 1024], r=['Rst'], w=['Rbf'])
                    for c in range(NMC):
                        ca = NH + c
                        t0 = ca * 128
                        us = lambda kt: uTm[:, kt * W + c * 128:kt * W + (c + 1) * 128]
                        dma('sp', cosc[:], cos_d[:, t0:t0 + 128], r=[], w=['cosc'])
                        dma('sp', sinc[:], sin_d[:, t0:t0 + 128], r=[], w=['sinc'])
                        tt('pool', cosq[:], cosc[:], glt[:, h * 128:(h + 1) * 128], ALU.mult, r=['cosc', 'gl'], w=['cosq'])
                        tt('pool', sinq[:], sinc[:], glt[:, h * 128:(h + 1) * 128], ALU.mult, r=['sinc', 'gl'], w=['sinq'])
                        for lt in range(4):
                            for kt in range(8):
                                mm(ps[0][:, lt * 128:(lt + 1) * 128], wt[:, kt * WG + lt * 128:kt * WG + (lt + 1) * 128], us(kt),
                                   kt == 0, kt == 7, r=[wk, 'uTm'], w=['ps0'])
                        for (o, cc_, sn_, dst) in [(0, cosq, sinq, qsT), (256, cosc, sinc, krm)]:
                            a1 = ps[0][:, o:o + 128]
                            a2 = ps[0][:, o + 128:o + 256]
                            tt('dve', rtm[0][:], a1, cc_[:], ALU.mult, r=['ps0', cc_.name], w=['rtm0'])
                            tt('dve', rtm[1][:], a2, sn_[:], ALU.mult, r=['ps0', sn_.name], w=['rtm1'])
                            tt('pool', dst[:, 0:128], rtm[0][:], rtm[1][:], ALU.subtract, r=['rtm0', 'rtm1'], w=[dst.name])
                            tt('dve', rtm[0][:], a2, cc_[:], ALU.mult, r=['ps0', cc_.name], w=['rtm0'])
                            tt('dve', rtm[1][:], a1, sn_[:], ALU.mult, r=['ps0', sn_.name], w=['rtm1'])
                            tt('pool', dst[:, 128:256], rtm[0][:], rtm[1][:], ALU.add, r=['rtm0', 'rtm1'], w=[dst.name])
                        for dti in range(2):
                            mm(ps[1][:, 0:128], krm[:, dti * 128:(dti + 1) * 128], qsT[:, dti * 128:(dti + 1) * 128], dti == 0, dti == 1,
                               r=['krm', 'qsT'], w=['ps1'])
                        tt('dve', PT[:, :], ps[1][:, 0:128], dmaskt[:, h * 128:(h + 1) * 128], ALU.mult, r=['ps1', 'dmask'], w=['PT'])
                        for kt in range(8):
                            mm(ps[2][:, :], us(kt), wt[:, kt * WG + 512:kt * WG + 1024], kt == 0, kt == 7, r=['uTm', wk], w=['ps2'])
                        for kt in range(8):
                            mm(ps[3][:, :], us(kt), wt[:, kt * WG + 1024:kt * WG + 1536], kt == 0, kt == 7, r=['uTm', wk], w=['ps3'])
                        cp('act', vbm[:, :], ps[2][:, :], r=['ps2'], w=['vbm'])
                        act(sgm[:, :], ps[3][:, :], AF.Silu, r=['ps3'], w=['sgm'])
                        for dti in range(2):
                            trn(psb[6][:, dti * 128:(dti + 1) * 128], krm[:, dti * 128:(dti + 1) * 128], r=['krm'], w=['ps6'])
                        if c == 0:
                            ts('dve', kdm[:, :], kdt[:, h:h + 1], maskc[:, ca:ca + 1], None, ALU.mult, None, r=['kd', 'maskc'], w=['kdmm'])
                            ksc = kdm[:, 0:1]
                        else:
                            ksc = kdt[:, h:h + 1]
                        act(kdecm[:, :], psb[6][:, 0:256], AF.Copy, r=['ps6', 'kdmm', 'kd'], w=['kdecm'], scale=ksc)
                        mm(ps[4][:, :], PT[:, :], vbm[:, :], True, False, r=['PT', 'vbm'], w=['ps4'])
                        for dti in range(2):
                            mm(ps[4][:, :], qsT[:, dti * 128:(dti + 1) * 128], Rbf[:, dti * 512:(dti + 1) * 512], False, dti == 1,
                               r=['qsT', 'Rbf'], w=['ps4'])
                        for dti in range(2):
                            bank = 5 + 2 * dti
                            mm(ps[bank][:, :], kdecm[:, dti * 128:(dti + 1) * 128], vbm[:, :], True, True, r=['kdecm', 'vbm'], w=[f'ps{bank}'])
                            ro = (h * 2 + dti) * 512
                            stt(Rst[:, ro:ro + 512], Rst[:, ro:ro + 512], float(cdec[h]), ps[bank][:, :], ALU.mult, ALU.add,
                                r=['Rst', f'ps{bank}'], w=['Rst'])
                        cp('act', Rbf[:, :], Rst[:, h * 1024:(h + 1) * 1024], r=['Rst'], w=['Rbf'])
                        act(sq2[:, :], ps[4][:, :], AF.Square, r=['ps4'], w=['sq2'])
                        rsum(ss[:, 0:1], sq2[:, :], r=['sq2'], w=['ss'])
                        act(ss[:, 0:1], ss[:, 0:1], AF.Sqrt, r=['ss'], w=['ss'], bias=EPS, scale=1.0 / 512)
                        recip(ss[:, 0:1], ss[:, 0:1], r=['ss'], w=['ss'])
                        stt(ynb[:, :], ps[4][:, :], ss[:, 0:1], sgm[:, :], ALU.mult, ALU.mult, r=['ps4', 'ss', 'sgm'], w=['ynb'])
                        yT = yTs[c % 2]
                        for lt in range(4):
                            trn(psb[6][:, 512 + lt * 128:512 + (lt + 1) * 128], ynb[:, lt * 128:(lt + 1) * 128], r=['ynb'], w=['ps6'])
                        cp('act', yT[:, :], psb[6][:, 512:1024], r=['ps6'], w=[yT.name])
                        dma('sp', yT_d[c * 128:(c + 1) * 128, (16 + 4 * h) * 128:(16 + 4 * h + 4) * 128], yT[:, :], r=[yT.name], w=[])
            S.barrier()
            st.close()
            st_state.close()

            with contextlib.ExitStack() as st3:
                Wg_ = sbuf(st3, "Wg", [128, 8 * 2048], BF16)
                Wbs = sbuf(st3, "Wbs", [128, 16 * 1024], BF16)
                Wbr = sbuf(st3, "Wbr", [128, 16 * 1024], BF16)
                Wo = sbuf(st3, "Wo", [128, 8 * 1024], BF16)
                yb = [sbuf(st3, "yb0", [128, 32 * 256], BF16)] * 2
                xres = [sbuf(st3, "xres0", [128, 8 * 256], F32)] * 2
                sgs = sbuf(st3, "sgs", [128, 256], F32)
                sgr = sbuf(st3, "sgr", [128, 256], F32)
                m1 = sbuf(st3, "m1", [128, 256], F32)
                m2 = sbuf(st3, "m2", [128, 256], F32)
                mrg = sbuf(st3, "mrg", [128, 8 * 256], BF16)
                h2o = [sbuf(st3, "h2o0", [128, 8 * 256], F32)] * 2
                for kt in range(8):
                    load_w('pool', Wg_, kt * 2048, w_in, kt * 128, CGS, 2048, 'Wg')
                    load_w('pool', Wo, kt * 1024, w_o, kt * 128, 0, 1024, 'Wo')
                for ct in range(16):
                    load_w('pool', Wbs, ct * 1024, w_bs, ct * 128, 0, 1024, 'Wbs')
                    load_w('pool', Wbr, ct * 1024, w_br, ct * 128, 0, 1024, 'Wbr')
                blocks = [(0, 1)] + [(1 + 2 * i, 2) for i in range(NM // 2)]
                for bi, (c0, ncnk) in enumerate(blocks if 'D1' in PH else []):
                    nt = ncnk * 128
                    ybb = yb[bi % 2]
                    xr = xres[bi % 2]
                    ho = h2o[bi % 2]
                    for j in range(ncnk):
                        dma('sp', ybb[:, 0:32 * nt].rearrange("p (ct t) -> p ct t", ct=32)[:, :, j * 128:(j + 1) * 128],
                            yT_d[(c0 + j) * 128:(c0 + j + 1) * 128, :].rearrange("p (ct t) -> p ct t", ct=32), r=[], w=[ybb.name])
                    ta = (NH + c0) * 128
                    dma('sp', xr[:, 0:8 * nt].rearrange("p (kt t) -> p kt t", kt=8),
                        xT_d[:, ta:ta + nt].rearrange("(kt p) t -> p kt t", p=128), r=[], w=[xr.name])
                    for dmt in range(8):
                        for kt in range(8):
                            mm(ps[0][:, 0:nt], Wg_[:, kt * 2048 + dmt * 128:kt * 2048 + (dmt + 1) * 128],
                               uTm[:, kt * W + c0 * 128:kt * W + c0 * 128 + nt], kt == 0, kt == 7, r=['Wg', 'uTm'], w=['ps0'])
                        for kt in range(8):
                            mm(ps[1][:, 0:nt], Wg_[:, kt * 2048 + 1024 + dmt * 128:kt * 2048 + 1024 + (dmt + 1) * 128],
                               uTm[:, kt * W + c0 * 128:kt * W + c0 * 128 + nt], kt == 0, kt == 7, r=['Wg', 'uTm'], w=['ps1'])
                        act(sgs[:, 0:nt], ps[0][:, 0:nt], AF.Sigmoid, r=['ps0'], w=['sgs'])
                        act(sgr[:, 0:nt], ps[1][:, 0:nt], AF.Sigmoid, r=['ps1'], w=['sgr'])
                        for ct in range(16):
                            mm(ps[2][:, 0:nt], Wbs[:, ct * 1024 + dmt * 128:ct * 1024 + (dmt + 1) * 128], ybb[:, ct * nt:(ct + 1) * nt],
                               ct == 0, ct == 15, r=['Wbs', ybb.name], w=['ps2'])
                        for ct in range(16):
                            mm(ps[3][:, 0:nt], Wbr[:, ct * 1024 + dmt * 128:ct * 1024 + (dmt + 1) * 128],
                               ybb[:, (16 + ct) * nt:(17 + ct) * nt], ct == 0, ct == 15, r=['Wbr', ybb.name], w=['ps3'])
                        tt('dve', m1[:, 0:nt], ps[2][:, 0:nt], sgs[:, 0:nt], ALU.mult, r=['ps2', 'sgs'], w=['m1'])
                        tt('dve', m2[:, 0:nt], ps[3][:, 0:nt], sgr[:, 0:nt], ALU.mult, r=['ps3', 'sgr'], w=['m2'])
                        tt('pool', mrg[:, dmt * nt:(dmt + 1) * nt], m1[:, 0:nt], m2[:, 0:nt], ALU.add, r=['m1', 'm2'], w=['mrg'])
                    for dmo in range(8):
                        bank = 4 + (dmo % 2)
                        for dmt in range(8):
                            mm(ps[bank][:, 0:nt], Wo[:, dmt * 1024 + dmo * 128:dmt * 1024 + (dmo + 1) * 128], mrg[:, dmt * nt:(dmt + 1) * nt],
                               dmt == 0, dmt == 7, r=['Wo', 'mrg'], w=[f'ps{bank}'])
                        tt('dve', ho[:, dmo * nt:(dmo + 1) * nt], ps[bank][:, 0:nt], xr[:, dmo * nt:(dmo + 1) * nt], ALU.add,
                           r=[f'ps{bank}', xr.name], w=[ho.name])
                    dma('sp', h2T_d[:, c0 * 128:c0 * 128 + nt].rearrange("(kt p) t -> p kt t", p=128),
                        ho[:, 0:8 * nt].rearrange("p (kt t) -> p kt t", kt=8), r=[ho.name], w=[])
                S.barrier()
            st_u.close()

        with contextlib.ExitStack() as st:
            Wup = sbuf(st, "Wup", [128, 8 * 2 * DFF], BF16)
            Wdn = sbuf(st, "Wdn", [128, 22 * 1024], BF16)
            h2b = [sbuf(st, f"h2b{i}", [128, 8 * 256], F32) for i in range(2)]
            sqb = sbuf(st, "sqbd", [128, 8 * 256], BF16)
            rstd = sbuf(st, "rstdd", [128, 256], F32)
            u2T = sbuf(st, "u2T", [128, 8 * 256], BF16)
            ab = [sbuf(st, f"ab{i}", [128, 258], F32) for i in range(2)]
            hal = sbuf(st, "hal", [128, 44 * 2], F32)
            cg = sbuf(st, "cg", [128, 256], F32)
            cv = sbuf(st, "cv", [128, 256], F32)
            sgt = sbuf(st, "sgt", [128, 256], F32)
            actT = sbuf(st, "actT", [128, 22 * 256], BF16)
            h3 = sbuf(st, "h3", [128, 8 * 256], F32)
            oT = [sbuf(st, f"oT{i}", [128, 8 * 256], F32) for i in range(2)]
            for kt in range(8):
                load_w('pool', Wup, kt * 2 * DFF, w_up, kt * 128, 0, 2 * DFF, 'Wup')
            for j in range(22):
                load_w('pool', Wdn, j * 1024, w_dn, j * 128, 0, 1024, 'Wdn')
            S.add('dve', lambda e: e.memset(hal[:], 0.0), w=['hal'])
            blocks = [(0, 1)] + [(1 + 2 * i, 2) for i in range(NM // 2)]
            for bi, (c0, ncnk) in enumerate(blocks if 'D2' in PH else []):
                nt = ncnk * 128
                hb = h2b[bi % 2]
                dma('sp', hb[:, 0:8 * nt].rearrange("p (kt t) -> p kt t", kt=8),
                    h2T_d[:, c0 * 128:c0 * 128 + nt].rearrange("(kt p) t -> p kt t", p=128), r=[], w=[hb.name])
                rms_block((sqb, rstd), hb, nt, w2T, [u2T[:, kt * nt:(kt + 1) * nt] for kt in range(8)], hb.name, 'u2T', 0,
                          mask_bc=(maskrow[:, 0:128] if bi == 0 else None))
                for j in range(22):
                    res = []
                    for vi, (colo, tile_i, dstc) in enumerate([(j * 128, j, cg), (DFF + j * 128, 22 + j, cv)]):
                        bank = 1 + vi
                        abuf = ab[vi]
                        for kt in range(8):
                            mm(ps[bank][:, 0:nt], Wup[:, kt * 2 * DFF + colo:kt * 2 * DFF + colo + 128], u2T[:, kt * nt:(kt + 1) * nt],
                               kt == 0, kt == 7, r=['Wup', 'u2T'], w=[f'ps{bank}'])
                        cp('pool', abuf[:, 0:2], hal[:, tile_i * 2:tile_i * 2 + 2], r=['hal'], w=[abuf.name])
                        cp('act', abuf[:, 2:2 + nt], ps[bank][:, 0:nt], r=[f'ps{bank}'], w=[abuf.name])
                        cp('pool', hal[:, tile_i * 2:tile_i * 2 + 2], abuf[:, nt:nt + 2], r=[abuf.name], w=['hal'])
                        ts('dve', dstc[:, 0:nt], abuf[:, 0:nt], fcw[:, tile_i * 3:tile_i * 3 + 1], fcb[:, tile_i:tile_i + 1], ALU.mult, ALU.add,
                           r=[abuf.name, 'fcw', 'fcb'], w=[dstc.name])
                        stt(dstc[:, 0:nt], abuf[:, 1:1 + nt], fcw[:, tile_i * 3 + 1:tile_i * 3 + 2], dstc[:, 0:nt], ALU.mult, ALU.add,
                            r=[abuf.name, 'fcw', dstc.name], w=[dstc.name])
                        stt(dstc[:, 0:nt], abuf[:, 2:2 + nt], fcw[:, tile_i * 3 + 2:tile_i * 3 + 3], dstc[:, 0:nt], ALU.mult, ALU.add,
                            r=[abuf.name, 'fcw', dstc.name], w=[dstc.name])
                    if bi > 0:
                        act(sgt[:, 0:nt], cg[:, 0:nt], AF.Silu, r=['cg'], w=['sgt'])
                        tt('pool', actT[:, j * nt:(j + 1) * nt], sgt[:, 0:nt], cv[:, 0:nt], ALU.mult, r=['sgt', 'cv'], w=['actT'])
                if bi == 0:
                    continue
                for dmt in range(8):
                    bank = 3 + (dmt % 2)
                    for j in range(22):
                        mm(ps[bank][:, 0:nt], Wdn[:, j * 1024 + dmt * 128:j * 1024 + (dmt + 1) * 128], actT[:, j * nt:(j + 1) * nt],
                           j == 0, j == 21, r=['Wdn', 'actT'], w=[f'ps{bank}'])
                    tt('dve', h3[:, dmt * nt:(dmt + 1) * nt], ps[bank][:, 0:nt], hb[:, dmt * nt:(dmt + 1) * nt], ALU.add,
                       r=[f'ps{bank}', hb.name], w=['h3'])
                ot = oT[bi % 2]
                rms_block((sqb, rstd), h3, nt, wfT, [ot[:, kt * nt:(kt + 1) * nt] for kt in range(8)], 'h3', ot.name, 5)
                dma('sp', outT_d[:, (c0 - 1) * 128:(c0 - 1) * 128 + nt].rearrange("(kt p) t -> p kt t", p=128),
                    ot[:, 0:8 * nt].rearrange("p (kt t) -> p kt t", kt=8), r=[ot.name], w=[])
            S.add('sp', None, extra_deps=list(S.dma_ops))
            with nc.Block() as block:
                S.emit(nc, block, sems, dma_sems)
    return nc, S


def make_inputs(x, meta_tokens, mix_norm_w, w_in, ssd_conv_w, ssd_conv_b, ssd_dt_bias, ssd_A_log, ssd_D, ssd_norm_w,
                w_branch_ssd, w_branch_ret, w_out, ffn_norm_w, w_up, ffn_conv_w, ffn_conv_b, w_down, final_norm_w):
    f32 = np.float32
    x = np.asarray(x, f32)
    B, SEQ, _ = x.shape
    NM = SEQ // 4 // 128
    NH = 3 * NM
    NALL = NH + 1 + NM
    L = NALL * 128
    T = NM * 128
    idx = np.arange(128)
    common = {}
    common['ident'] = np.eye(128, dtype=f32)
    common['tri'] = (idx[:, None] <= idx[None, :]).astype(f32)
    common['sut'] = (idx[:, None] > idx[None, :]).astype(f32)
    lg = np.log(1.0 - 2.0 ** (-5.0 - np.arange(4, dtype=np.float64)))
    dm = np.zeros((128, 4, 128), np.float64)
    gl = np.zeros((128, 4, 128), np.float64)
    kd = np.zeros((128, 4), np.float64)
    for h in range(4):
        dm[:, h, :] = np.where(idx[:, None] <= idx[None, :], np.exp(-(idx[:, None] + 1.0) * lg[h]) / 16.0, 0.0)
        gl[:, h, :] = np.exp((idx[None, :] + 1.0) * lg[h])
        kd[:, h] = np.exp((127.0 - idx) * lg[h]) / 16.0
    common['dmask'] = dm.reshape(128, 512).astype(f32)
    common['gl'] = gl.reshape(128, 512).astype(f32)
    common['kd'] = kd.astype(f32)
    colT = lambda v: np.ascontiguousarray(np.asarray(v, f32).reshape(-1, 128).T)
    common['w1T'] = colT(mix_norm_w[0])
    common['w2T'] = colT(ffn_norm_w[0])
    common['wfT'] = colT(final_norm_w)
    cw = np.asarray(ssd_conv_w[0], f32)
    common['convw'] = np.ascontiguousarray(cw.reshape(4, 24, 128).transpose(2, 1, 0).reshape(128, 96))
    common['convb'] = colT(ssd_conv_b[0])
    common['dtb'] = np.asarray(ssd_dt_bias[0], f32).reshape(1, 32)
    common['alog'] = np.asarray(ssd_A_log[0], f32).reshape(1, 32)
    common['dsk'] = np.asarray(ssd_D[0], f32).reshape(1, 32)
    common['normw'] = np.asarray(ssd_norm_w[0], f32).reshape(1, 2048)
    fw = np.asarray(ffn_conv_w[0], f32)
    common['fcw'] = np.ascontiguousarray(fw.reshape(3, 44, 128).transpose(2, 1, 0).reshape(128, 132))
    common['fcb'] = colT(ffn_conv_b[0])
    common['w_in'] = np.ascontiguousarray(np.asarray(w_in[0], f32))
    common['w_bs'] = np.ascontiguousarray(np.asarray(w_branch_ssd[0], f32))
    common['w_br'] = np.ascontiguousarray(np.asarray(w_branch_ret[0], f32))
    common['w_o'] = np.ascontiguousarray(np.asarray(w_out[0], f32))
    common['w_up'] = np.ascontiguousarray(np.asarray(w_up[0], f32))
    common['w_dn'] = np.ascontiguousarray(np.asarray(w_down[0], f32))
    inv_freq = (10000.0 ** (-np.linspace(0.0, 1.0, 128, dtype=f32))).astype(f32)
    in_maps = []
    meta = np.asarray(meta_tokens, f32)
    for b in range(B):
        seqp = np.zeros((128 + SEQ, D), f32)
        seqp[112:128] = meta
        seqp[128:] = x[b]
        for k in range(4):
            p0 = k * T - NH * 128
            ps_ = np.arange(p0, p0 + L)
            valid = ps_ >= 0
            win = np.zeros((L, D), f32)
            win[valid] = seqp[ps_[valid]]
            m = dict(common)
            m['xT'] = np.ascontiguousarray(win.T)
            pos = (ps_ - 112).astype(f32)
            ang = pos[None, :] * inv_freq[:, None]
            m['cosT'] = np.cos(ang).astype(f32)
            m['sinT'] = np.sin(ang).astype(f32)
            mask = (ps_ >= 112).astype(f32)
            m['maskc'] = np.ascontiguousarray(mask.reshape(NALL, 128).T)
            m['maskrow'] = np.ascontiguousarray(np.broadcast_to(mask[NH * 128:(NH + 1) * 128][None, :], (128, 128))).astype(f32)
            in_maps.append({'c_' + kk: vv for kk, vv in m.items()})
    return in_maps, NM


_CACHE = {}


def kernel(**inputs):
    in_maps, NM = make_inputs(**inputs)
    if NM not in _CACHE:
        _CACHE[NM] = build(NM)[0]
    nc = _CACHE[NM]
    res = run_bass_kernel_spmd(nc, in_maps, core_ids=list(range(len(in_maps))))
    x = inputs['x']
    B, SEQ, _ = x.shape
    T = NM * 128
    out = np.zeros((B, SEQ, D), np.float32)
    for b in range(B):
        for k in range(4):
            out[b, k * T:(k + 1) * T] = np.asarray(res.results[b * 4 + k]["outT"], np.float32).T
    return out
```
